# Optimizing a Trainium2 kernel written in Bass

```python
import math
import jax, jax.numpy as jnp
from jax import lax
import numpy as np

D_MODEL = 2048
BATCH = 32
SEQ = 256
DEPTH = 4
DEC_BATCH = 2
DEC_SEQ = 4096
PAST_LEN = 256

GRID_W = 64
HEAD_DIM = 128
MIX_W = D_MODEL
ATT_W = MIX_W // 2
ML_W = MIX_W // 4
DF_W = MIX_W // 4
ATT_HEADS = ATT_W // HEAD_DIM
ATT_KV_HEADS = 2
KV_W = ATT_KV_HEADS * HEAD_DIM
MLSTM_HEADS = ML_W // HEAD_DIM
DIFF_HEADS = DF_W // HEAD_DIM
DIFF_QK_DIM = HEAD_DIM // 2
N_DIR = 2
N_GATES = N_DIR * 2 * MLSTM_HEADS
D_FF = 4 * D_MODEL
QBLOCK = 128
MLSTM_CHUNK = 128
ROPE_THETA = 10000.0
EPS = 1e-6
SPLITS = (ATT_W, KV_W, KV_W, ML_W, ML_W, ML_W, ML_W, N_GATES, DF_W, DF_W, DF_W)
IN_W = sum(SPLITS)

kernel_name = 'hybrid_gqa_mlstm_diffattn_dit_step'


def rms_norm(x, g):
    xf = x.astype(jnp.float32)
    y = xf * lax.rsqrt(jnp.mean(xf * xf, axis=-1, keepdims=True) + EPS)
    return (y * g.astype(jnp.float32)).astype(x.dtype)


def axial_rope(n_tok, dim):
    rows = n_tok // GRID_W
    row_idx = jnp.repeat(jnp.arange(rows), GRID_W).astype(jnp.float32)
    col_idx = jnp.tile(jnp.arange(GRID_W), rows).astype(jnp.float32)
    n_freq = dim // 4
    inv = ROPE_THETA ** (-jnp.arange(n_freq, dtype=jnp.float32) / n_freq)
    ang = jnp.concatenate([row_idx[:, None] * inv, col_idx[:, None] * inv], axis=-1)
    return jnp.cos(ang), jnp.sin(ang)


def apply_rope(x, cos, sin):
    xf = x.astype(jnp.float32)
    half = x.shape[-1] // 2
    x1, x2 = xf[..., :half], xf[..., half:]
    c, s = cos[None, :, None, :], sin[None, :, None, :]
    return jnp.concatenate([x1 * c - x2 * s, x1 * s + x2 * c], axis=-1).astype(x.dtype)


def rope_two_maps(x, cos, sin):
    return jnp.concatenate([apply_rope(x[..., :DIFF_QK_DIM], cos, sin),
                            apply_rope(x[..., DIFF_QK_DIM:], cos, sin)], axis=-1)


def sweep_query_blocks(fn, *qs):
    b, n = qs[0].shape[:2]
    nb = n // QBLOCK
    blocks = tuple(jnp.moveaxis(q.reshape(b, nb, QBLOCK, *q.shape[2:]), 1, 0) for q in qs)
    out = lax.map(lambda blk: fn(*blk), blocks)
    out = jnp.moveaxis(out, 0, 1)
    return out.reshape(b, n, *out.shape[3:])


def gqa_attention(q, k, v):
    g = k.shape[2]
    scale = q.shape[-1] ** -0.5

    def block(qb):
        b, nq, h, d = qb.shape
        qg = qb.reshape(b, nq, g, h // g, d)
        s = jnp.einsum('bqgrd,btgd->bgrqt', qg, k).astype(jnp.float32) * scale
        p = jax.nn.softmax(s, axis=-1).astype(v.dtype)
        o = jnp.einsum('bgrqt,btgd->bqgrd', p, v)
        return o.reshape(b, nq, h, d)

    return sweep_query_blocks(block, q)


def diff_attention(q1, q2, k1, k2, v, lam):
    scale = q1.shape[-1] ** -0.5

    def block(qb1, qb2):
        s1 = jnp.einsum('bqhd,bthd->bhqt', qb1, k1).astype(jnp.float32) * scale
        s2 = jnp.einsum('bqhd,bthd->bhqt', qb2, k2).astype(jnp.float32) * scale
        p = jax.nn.softmax(s1, axis=-1) - lam * jax.nn.softmax(s2, axis=-1)
        return jnp.einsum('bhqt,bthe->bqhe', p.astype(v.dtype), v)

    return sweep_query_blocks(block, q1, q2)


def mlstm_scan(q, k, v, ig, lf, state):
    b, s, h, d = q.shape
    nc = s // MLSTM_CHUNK
    causal = jnp.tril(jnp.ones((MLSTM_CHUNK, MLSTM_CHUNK), dtype=bool))

    def to_chunks(a):
        return jnp.moveaxis(a.reshape(b, nc, MLSTM_CHUNK, *a.shape[2:]), 1, 0)

    def step(carry, xs):
        C, n, m = carry
        qc, kc, vc, ic, fc = xs
        bcum = jnp.cumsum(fc, axis=1)
        logw = bcum[:, :, None, :] - bcum[:, None, :, :] + ic[:, None, :, :]
        logw = jnp.where(causal[None, :, :, None], logw, -jnp.inf)
        inter = bcum + m[:, None, :]
        m_row = jnp.maximum(inter, jnp.max(logw, axis=2))
        sc = jnp.einsum('bjhd,bshd->bjsh', qc, kc) * jnp.exp(logw - m_row[:, :, None, :])
        a_inter = jnp.exp(inter - m_row)
        num = (jnp.einsum('bjsh,bshe->bjhe', sc, vc)
               + a_inter[..., None] * jnp.einsum('bjhd,bhde->bjhe', qc, C))
        den = jnp.sum(sc, axis=2) + a_inter * jnp.einsum('bjhd,bhd->bjh', qc, n)
        hc = num / jnp.maximum(jnp.abs(den), jnp.exp(-m_row))[..., None]
        m_new = m_row[:, -1]
        w_end = jnp.exp(bcum[:, -1:, :] - bcum + ic - m_new[:, None, :])
        decay = jnp.exp(bcum[:, -1] + m - m_new)
        C_new = decay[..., None, None] * C + jnp.einsum('bsh,bshd,bshe->bhde', w_end, kc, vc)
        n_new = decay[..., None] * n + jnp.einsum('bsh,bshd->bhd', w_end, kc)
        return (C_new, n_new, m_new), hc

    state, hs = lax.scan(step, state, tuple(to_chunks(a) for a in (q, k, v, ig, lf)))
    hs = jnp.moveaxis(hs, 0, 1).reshape(b, s, h, d)
    return hs, state


def mlstm_bidir(q, k, v, gates, st_f, st_b):
    flip = lambda a: jnp.flip(a, axis=1)
    ig = gates[:, :, :, 0]
    lf = jax.nn.log_sigmoid(gates[:, :, :, 1])
    h_f, st_f = mlstm_scan(q, k, v, ig[:, :, 0], lf[:, :, 0], st_f)
    h_b, st_b = mlstm_scan(flip(q), flip(k), flip(v), flip(ig[:, :, 1]), flip(lf[:, :, 1]), st_b)
    return h_f + flip(h_b), st_f, st_b


def mixing(hn, lp, layer, ctx):
    f32 = jnp.float32
    b, n, _ = hn.shape
    proj = hn @ lp['w_in']
    idx = np.cumsum(SPLITS)[:-1].tolist()
    aq, ak, av, mq, mk, mv, mo, mg, dq, dk, dv = jnp.split(proj, idx, axis=-1)
    heads = lambda a, h: a.reshape(b, n, h, HEAD_DIM)
    aq = rms_norm(heads(aq, ATT_HEADS), lp['qk_gain'][0])
    ak = rms_norm(heads(ak, ATT_KV_HEADS), lp['qk_gain'][1])
    av = heads(av, ATT_KV_HEADS)
    dq, dk, dv = heads(dq, DIFF_HEADS), heads(dk, DIFF_HEADS), heads(dv, DIFF_HEADS)
    mq = heads(mq, MLSTM_HEADS).astype(f32)
    mk = heads(mk, MLSTM_HEADS).astype(f32) * HEAD_DIM ** -0.5
    mv = heads(mv, MLSTM_HEADS).astype(f32)
    gates = mg.astype(f32).reshape(b, n, N_DIR, 2, MLSTM_HEADS) + lp['gate_bias'].astype(f32)

    if ctx is None:
        q_att, k_att, v_att = aq, ak, av
        q_dif, k_dif, v_dif = dq, dk, dv
        zero = (jnp.zeros((b, MLSTM_HEADS, HEAD_DIM, HEAD_DIM), f32),
                jnp.zeros((b, MLSTM_HEADS, HEAD_DIM), f32),
                jnp.full((b, MLSTM_HEADS), -jnp.inf, f32))
        st_f, st_b = zero, zero
    else:
        cos, sin = axial_rope(n, HEAD_DIM)
        cos2, sin2 = axial_rope(n, DIFF_QK_DIM)
        q_att = apply_rope(aq, cos, sin)
        k_att = jnp.concatenate([apply_rope(ak, cos, sin), ctx['gqa_k'].astype(ak.dtype)], axis=1)
        v_att = jnp.concatenate([av, ctx['gqa_v'].astype(av.dtype)], axis=1)
        q_dif = rope_two_maps(dq, cos2, sin2)
        k_dif = jnp.concatenate([rope_two_maps(dk, cos2, sin2), ctx['diff_k'].astype(dk.dtype)], axis=1)
        v_dif = jnp.concatenate([dv, ctx['diff_v'].astype(dv.dtype)], axis=1)
        st_f = (ctx['C'][:, 0].astype(f32), ctx['n'][:, 0].astype(f32), ctx['m'][:, 0].astype(f32))
        st_b = (ctx['C'][:, 1].astype(f32), ctx['n'][:, 1].astype(f32), ctx['m'][:, 1].astype(f32))

    att_out = gqa_attention(q_att, k_att, v_att)

    lam_init = 0.8 - 0.6 * math.exp(-0.3 * layer)
    lv = lp['diff_lambda'].astype(f32)
    lam = jnp.exp(jnp.sum(lv[0] * lv[1])) - jnp.exp(jnp.sum(lv[2] * lv[3])) + lam_init
    dif_out = diff_attention(q_dif[..., :DIFF_QK_DIM], q_dif[..., DIFF_QK_DIM:],
                             k_dif[..., :DIFF_QK_DIM], k_dif[..., DIFF_QK_DIM:], v_dif, lam)
    dif_out = rms_norm(dif_out, lp['diff_gain']) * (1.0 - lam_init)

    h_ml, st_f, st_b = mlstm_bidir(mq, mk, mv, gates, st_f, st_b)
    o_gate = jax.nn.sigmoid(mo.astype(f32)).reshape(b, n, MLSTM_HEADS, HEAD_DIM)
    h_ml = rms_norm(h_ml, lp['ml_gain']) * o_gate

    cat = jnp.concatenate([att_out.reshape(b, n, ATT_W),
                           h_ml.reshape(b, n, ML_W).astype(hn.dtype),
                           dif_out.reshape(b, n, DF_W)], axis=-1)
    out = cat @ lp['w_out']
    if ctx is None:
        dt = hn.dtype
        new_ctx = (ak, av, dk, dv,
                   jnp.stack([st_f[0], st_b[0]], axis=1).astype(dt),
                   jnp.stack([st_f[1], st_b[1]], axis=1).astype(dt),
                   jnp.stack([st_f[2], st_b[2]], axis=1).astype(dt))
    else:
        new_ctx = None
    return out, new_ctx


def trunk_layer(x, cond, lp, layer, ctx):
    mod = jax.nn.silu(cond) @ lp['w_ada'] + lp['b_ada']
    sh1, sc1, g1, sh2, sc2, g2 = (m[:, None, :] for m in jnp.split(mod, 6, axis=-1))
    g = lp['norm_gain']
    hn = rms_norm(x, g[0]) * (1 + sc1) + sh1
    mix, new_ctx = mixing(hn, lp, layer, ctx)
    x = x + g1 * rms_norm(mix, g[1])
    hn = rms_norm(x, g[2]) * (1 + sc2) + sh2
    ff = jnp.square(jax.nn.relu(hn @ lp['w_ff1'])) @ lp['w_ff2']
    x = x + g2 * rms_norm(ff, g[3])
    return x, new_ctx


def setup_inputs(seed: int = 0) -> dict:
    key = jax.random.key(seed)
    ks = jax.random.split(key, 24)
    f32 = jnp.float32
    nrm = lambda k, shape, s=1.0: jax.random.normal(k, shape, f32) * s
    gate_base = jnp.array([0.0, 3.0], f32)[None, None, :, None]
    gate_scale = jnp.array([0.1, 0.5], f32)[None, None, :, None]
    return {
        'x_prompt': nrm(ks[0], (BATCH, SEQ, D_MODEL)),
        'x_sample': nrm(ks[1], (DEC_BATCH, DEC_SEQ, D_MODEL)),
        'c': nrm(ks[2], (DEC_BATCH, D_MODEL)),
        'cache_gqa_k': nrm(ks[3], (DEC_BATCH, DEPTH, PAST_LEN, ATT_KV_HEADS, HEAD_DIM)),
        'cache_gqa_v': nrm(ks[4], (DEC_BATCH, DEPTH, PAST_LEN, ATT_KV_HEADS, HEAD_DIM)),
        'cache_diff_k': nrm(ks[5], (DEC_BATCH, DEPTH, PAST_LEN, DIFF_HEADS, HEAD_DIM)),
        'cache_diff_v': nrm(ks[6], (DEC_BATCH, DEPTH, PAST_LEN, DIFF_HEADS, HEAD_DIM)),
        'state_mlstm_C': nrm(ks[7], (DEC_BATCH, DEPTH, N_DIR, MLSTM_HEADS, HEAD_DIM, HEAD_DIM), 0.1),
        'state_mlstm_n': nrm(ks[8], (DEC_BATCH, DEPTH, N_DIR, MLSTM_HEADS, HEAD_DIM), 0.1),
        'state_mlstm_m': nrm(ks[9], (DEC_BATCH, DEPTH, N_DIR, MLSTM_HEADS), 0.5),
        'c_ctx': nrm(ks[10], (D_MODEL,)),
        'w_ada': nrm(ks[11], (DEPTH, D_MODEL, 6 * D_MODEL), 0.5 * D_MODEL ** -0.5),
        'b_ada': nrm(ks[12], (DEPTH, 6 * D_MODEL), 0.02),
        'norm_gain': 1.0 + nrm(ks[13], (DEPTH, 4, D_MODEL), 0.02),
        'w_in': nrm(ks[14], (DEPTH, D_MODEL, IN_W), D_MODEL ** -0.5),
        'w_out': nrm(ks[15], (DEPTH, MIX_W, D_MODEL), MIX_W ** -0.5),
        'qk_gain': 1.0 + nrm(ks[16], (DEPTH, 2, HEAD_DIM), 0.02),
        'mlstm_gate_bias': gate_base + gate_scale * nrm(ks[17], (DEPTH, N_DIR, 2, MLSTM_HEADS)),
        'mlstm_head_gain': 1.0 + nrm(ks[18], (DEPTH, MLSTM_HEADS, HEAD_DIM), 0.02),
        'diff_lambda': nrm(ks[19], (DEPTH, 4, DIFF_QK_DIM), 0.1),
        'diff_head_gain': 1.0 + nrm(ks[20], (DEPTH, HEAD_DIM), 0.02),
        'w_ff1': nrm(ks[21], (DEPTH, D_MODEL, D_FF), D_MODEL ** -0.5),
        'w_ff2': nrm(ks[22], (DEPTH, D_FF, D_MODEL), D_FF ** -0.5),
    }


def reference(x_prompt, x_sample, c, cache_gqa_k, cache_gqa_v, cache_diff_k, cache_diff_v,
              state_mlstm_C, state_mlstm_n, state_mlstm_m, c_ctx, w_ada, b_ada, norm_gain,
              w_in, w_out, qk_gain, mlstm_gate_bias, mlstm_head_gain, diff_lambda,
              diff_head_gain, w_ff1, w_ff2):
    def layer_params(l):
        return {'w_ada': w_ada[l], 'b_ada': b_ada[l], 'norm_gain': norm_gain[l],
                'w_in': w_in[l], 'w_out': w_out[l], 'qk_gain': qk_gain[l],
                'gate_bias': mlstm_gate_bias[l], 'ml_gain': mlstm_head_gain[l],
                'diff_lambda': diff_lambda[l], 'diff_gain': diff_head_gain[l],
                'w_ff1': w_ff1[l], 'w_ff2': w_ff2[l]}

    xp = x_prompt
    collected = [[] for _ in range(7)]
    for l in range(DEPTH):
        xp, ctx_t = trunk_layer(xp, c_ctx[None, :], layer_params(l), l, None)
        for lst, t in zip(collected, ctx_t):
            lst.append(t)
    y_prompt = xp
    new_gqa_k, new_gqa_v, new_diff_k, new_diff_v, new_mlstm_C, new_mlstm_n, new_mlstm_m = (
        jnp.stack(lst, axis=1) for lst in collected)

    xs = x_sample
    for l in range(DEPTH):
        ctx = {'gqa_k': cache_gqa_k[:, l], 'gqa_v': cache_gqa_v[:, l],
               'diff_k': cache_diff_k[:, l], 'diff_v': cache_diff_v[:, l],
               'C': state_mlstm_C[:, l], 'n': state_mlstm_n[:, l], 'm': state_mlstm_m[:, l]}
        xs, _ = trunk_layer(xs, c, layer_params(l), l, ctx)
    y_sample = xs

    return (y_prompt, y_sample, new_gqa_k, new_gqa_v, new_diff_k, new_diff_v,
            new_mlstm_C, new_mlstm_n, new_mlstm_m)
```

```python
import math
import contextlib
import numpy as np
import concourse.bass as bass
import concourse.mybir as mybir
from concourse.bass_utils import run_bass_kernel_spmd

F32 = mybir.dt.float32
BF16 = mybir.dt.bfloat16
AF = mybir.ActivationFunctionType
ALU = mybir.AluOpType
AX = mybir.AxisListType

D = 2048
NL = 4
TP = 1024
TS = 4096
TKS = TS + 256
EPS = 1e-6
NEG = -1.0e30
SCALE = 128.0 ** -0.5
DSCALE = 64.0 ** -0.5
C_AQ, C_AK, C_AV, C_MQ, C_MK, C_MV, C_MO, C_MG, C_DQ, C_DK, C_DV = 0, 1024, 1280, 1536, 2048, 2560, 3072, 3584, 3600, 4112, 4624


class Buf:
    __slots__ = ("name", "lw", "rd", "excl")

    def __init__(self, name):
        self.name = name
        self.excl = name.startswith("ps")
        self.lw = None
        self.rd = {}


class Prog:
    ENGS = ("pe", "act", "dve", "pool", "sp")

    def __init__(self, nc, n_dma_sems=24, n_sw=6):
        self.nc = nc
        self.items = {e: [] for e in self.ENGS}
        self.count = {e: 0 for e in self.ENGS}
        self.signal = {e: set() for e in self.ENGS}
        self.known = {e: {} for e in self.ENGS}
        self.known_d = {e: {} for e in self.ENGS}
        self.n_dma_sems = n_dma_sems
        self.dma_val = [0] * n_dma_sems
        self.n_sw = n_sw
        self.rr_hw = 0
        self.rr_sw = 0

    def _add_wait(self, eng, tok, waits):
        if tok is None:
            return
        if tok[0] == 'E':
            _, src, idx = tok
            if src == eng and eng == 'pe':
                return
            if self.known[eng].get(src, 0) >= idx:
                return
            if idx > waits.get(('E', src), 0):
                waits[('E', src)] = idx
        else:
            _, s, val = tok
            if self.known_d[eng].get(s, 0) >= val:
                return
            if val > waits.get(('D', s), 0):
                waits[('D', s)] = val

    def _apply(self, eng, waits):
        for k, v in waits.items():
            if k[0] == 'E':
                self.known[eng][k[1]] = v
                self.signal[k[1]].add(v)
            else:
                self.known_d[eng][k[1]] = v

    def _collect(self, eng, reads, writes):
        waits = {}
        for b in reads:
            self._add_wait(eng, b.lw, waits)
            if b.excl:
                for key, t in b.rd.items():
                    if key != ('E', eng):
                        self._add_wait(eng, t, waits)
        for b in writes:
            self._add_wait(eng, b.lw, waits)
            for t in b.rd.values():
                self._add_wait(eng, t, waits)
        self._apply(eng, waits)
        return waits

    def _commit(self, tok, rkey, reads, writes):
        for b in reads:
            b.rd[rkey] = tok
        for b in writes:
            b.lw = tok
            b.rd = {}

    def op(self, eng, fn, reads=(), writes=()):
        waits = self._collect(eng, reads, writes)
        self.count[eng] += 1
        idx = self.count[eng]
        self.items[eng].append((waits, fn, idx, None))
        self._commit(('E', eng, idx), ('E', eng), reads, writes)

    def dma(self, q, fn, reads=(), writes=()):
        if q == 'pool':
            s = self.rr_sw
            self.rr_sw = (self.rr_sw + 1) % self.n_sw
        else:
            s = self.n_sw + self.rr_hw
            self.rr_hw = (self.rr_hw + 1) % (self.n_dma_sems - self.n_sw)
        waits = self._collect(q, reads, writes)
        prev = self.dma_val[s]
        if prev > 0 and self.known_d[q].get(s, 0) < prev:
            waits[('D', s)] = max(waits.get(('D', s), 0), prev)
            self.known_d[q][s] = prev
        self.dma_val[s] = prev + 16
        self.count[q] += 1
        self.items[q].append((waits, fn, self.count[q], s))
        self._commit(('D', s, prev + 16), ('D', s), reads, writes)

    def barrier(self):
        toks = [('E', e, self.count[e]) for e in ('pe', 'act', 'dve') if self.count[e] > 0]
        toks += [('D', s, v) for s, v in enumerate(self.dma_val) if v > 0]
        for e in self.ENGS:
            waits = {}
            for t in toks:
                self._add_wait(e, t, waits)
            self._apply(e, waits)
            if waits:
                self.items[e].append((waits, None, None, None))

    def emit(self):
        nc = self.nc
        with contextlib.ExitStack() as st:
            esem = {e: st.enter_context(nc.semaphore("s_" + e)) for e in self.ENGS}
            dsem = [st.enter_context(nc.semaphore("d_%d" % i)) for i in range(self.n_dma_sems)]
            block = st.enter_context(nc.Block())
            ordmap = {}
            for e in self.ENGS:
                ordmap[e] = {k: i + 1 for i, k in enumerate(sorted(self.signal[e]))}

            def run(ename, eobj):
                for waits, fn, idx, ds in self.items[ename]:
                    for k, v in waits.items():
                        if k[0] == 'E':
                            eobj.wait_ge(esem[k[1]], ordmap[k[1]][v])
                        else:
                            eobj.wait_ge(dsem[k[1]], v)
                    if fn is None:
                        continue
                    ins = fn(eobj)
                    if ds is not None:
                        ins.then_inc(dsem[ds], 16)
                    elif idx in ordmap[ename]:
                        ins.then_inc(esem[ename], 1)
                if ename == 'sp':
                    for i in range(self.n_dma_sems):
                        if self.dma_val[i] > 0:
                            eobj.wait_ge(dsem[i], self.dma_val[i])

            @block.tensor
            def _(e):
                run('pe', e)

            @block.scalar
            def _(e):
                run('act', e)

            @block.vector
            def _(e):
                run('dve', e)

            @block.gpsimd
            def _(e):
                run('pool', e)

            @block.sync
            def _(e):
                run('sp', e)


class TT:
    def __init__(self, t, name, b=None):
        self.t = t
        self.b = b if b is not None else Buf(name)

    def __getitem__(self, idx):
        return self.t[idx]


class Ring:
    def __init__(self, tiles):
        self.tiles = tiles
        self.i = 0

    def get(self):
        t = self.tiles[self.i]
        self.i = (self.i + 1) % len(self.tiles)
        return t


def build_program(nl_run=NL):
    nc = bass.Bass("TRN2", target_bir_lowering=False)
    P = Prog(nc)

    def DIN(name, shape, dt=F32):
        return nc.dram_tensor(name, list(shape), dt, kind="ExternalInput").ap()

    def DOUT(name, shape, dt=F32):
        return nc.dram_tensor(name, list(shape), dt, kind="ExternalOutput").ap()

    def DSC(name, shape, dt=F32):
        return TT(nc.dram_tensor(name, list(shape), dt, kind="Internal").ap(), name)

    xin = {'p': DIN("xp", [TP, D]), 's': DIN("xs", [TS, D])}
    condT = DIN("condT", [128, 16, 2])
    w_ada = DIN("w_ada", [NL, D, 6 * D]); w_in = DIN("w_in", [NL, D, 5136]); w_out = DIN("w_out", [NL, D, D])
    w_ff1 = DIN("w_ff1", [NL, D, 4 * D]); w_ff2 = DIN("w_ff2", [NL, 4 * D, D])
    b_ada_fm = DIN("b_ada_fm", [NL, 128, 96]); b_ada_rows = DIN("b_ada_rows", [NL, 2, 2, D])
    ng_fm = DIN("ng_fm", [NL, 128, 4, 16]); ng_rows = DIN("ng_rows", [NL, 2, 2, D])
    qkg_col = DIN("qkg_col", [NL, 128, 2]); kg_bc = DIN("kg_bc", [NL, 128, 128])
    gb16_in = DIN("gb16", [NL, 128, 1]); mlg_col = DIN("mlg_col", [NL, 128, 4]); dg_col = DIN("dg_col", [NL, 128, 1])
    dl_bc = DIN("dl_bc", [NL, 128, 256]); minit_s = DIN("minit_s", [NL, 128, 2])
    cgk = DIN("cgk", [NL, 256, 256]); cgv = DIN("cgv", [NL, 256, 256]); cdk = DIN("cdk", [NL, 256, 512]); cdv = DIN("cdv", [NL, 256, 512])
    stC = DIN("stC", [NL, 2, 4, 128, 128]); stn = DIN("stn", [NL, 2, 4, 128, 1])
    c_ident = DIN("c_ident", [128, 128]); c_mask = DIN("c_mask", [2, 128, 128]); c_tri = DIN("c_tri", [2, 128, 128])
    c_sel = DIN("c_sel", [2, 128, 128]); c_perm = DIN("c_perm", [2, 128, 128])
    c_rope = DIN("c_rope", [4, 128, TS])
    yout = {'p': DOUT("yp", [TP, D]), 's': DOUT("ys", [TS, D])}
    o_gk = DOUT("ngk", [4, NL, 256, 256]); o_gv = DOUT("ngv", [4, NL, 256, 256])
    o_dk = DOUT("ndk", [4, NL, 256, 512]); o_dv = DOUT("ndv", [4, NL, 256, 512])
    o_C = DOUT("nC", [4, NL, 2, 4, 128, 128]); o_n = DOUT("nn", [4, NL, 2, 4, 128, 1]); o_m = DOUT("nm", [4, NL, 2, 4, 1])
    bout = Buf("outs")

    TG = {'p': TP, 's': TS}
    TK = {'p': TP, 's': TKS}
    X = {g: DSC("X_" + g, [TG[g], D]) for g in 'ps'}
    X1 = {g: DSC("X1_" + g, [TG[g], D]) for g in 'ps'}
    QT = {g: DSC("QT_" + g, [8, 128, TG[g]], BF16) for g in 'ps'}
    KT = {g: DSC("KT_" + g, [2, 128, TK[g]], BF16) for g in 'ps'}
    VV = {g: DSC("V_" + g, [TK[g], 256], BF16) for g in 'ps'}
    DQT = {g: DSC("DQT_" + g, [4, 128, TG[g]], BF16) for g in 'ps'}
    DKT = {g: DSC("DKT_" + g, [4, 128, TK[g]], BF16) for g in 'ps'}
    DV = {g: DSC("DV_" + g, [TK[g], 512], BF16) for g in 'ps'}
    MQT = {g: DSC("MQT_" + g, [4, 128, TG[g]], BF16) for g in 'ps'}
    MKT = {g: DSC("MKT_" + g, [4, 128, TG[g]], BF16) for g in 'ps'}
    MK = {g: DSC("MK_" + g, [TG[g], 512], BF16) for g in 'ps'}
    MV = {g: DSC("MV_" + g, [TG[g], 512], BF16) for g in 'ps'}
    OGT = {g: DSC("OGT_" + g, [4, 128, TG[g]], BF16) for g in 'ps'}
    GR = {g: DSC("GR_" + g, [4, TG[g] // 128, 4, 128]) for g in 'ps'}
    CATT = {g: DSC("CATT_" + g, [16, 128, TG[g]], BF16) for g in 'ps'}
    GBS = DSC("GBS", [2, 2, 128, D])
    BNC = DSC("BNC", [8, 128])

    with contextlib.ExitStack() as st:
        def SB(name, shape, dt=F32):
            return TT(st.enter_context(nc.sbuf_tensor(name, list(shape), dt)), name)

        arena_b = st.enter_context(nc.sbuf_tensor("arena_b", [128, 38912], BF16))
        arena_f = st.enter_context(nc.sbuf_tensor("arena_f", [128, 16384], F32))
        WR = Ring([TT(arena_b[:, i * 4096:(i + 1) * 4096], "w%d" % i) for i in range(3)])
        hnT = TT(arena_b[:, 12288:20480].rearrange("p (k t) -> p k t", k=16), "hnT")
        junk = TT(arena_b[:, 20480:22528], "junk")
        hT = TT(arena_b[:, 22528:38912].rearrange("p (k t) -> p k t", k=32), "hT")
        pb0 = 12288
        ktile = TT(arena_b[:, pb0:pb0 + 4352], "ktile")
        vtile = TT(arena_b[:, pb0 + 4352:pb0 + 8704].rearrange("p (n d) -> p n d", d=128), "vtile")
        qA = TT(arena_b[:, pb0 + 8704:pb0 + 12800], "qA")
        qB = TT(arena_b[:, pb0 + 12800:pb0 + 16896], "qB")
        mkt_t = TT(arena_b[:, pb0:pb0 + 4096], "mkt")
        mqt_t = TT(arena_b[:, pb0 + 4096:pb0 + 8192], "mqt")
        mk_t = TT(arena_b[:, pb0 + 8192:pb0 + 12288].rearrange("p (n d) -> p n d", d=128), "mk")
        mv_t = TT(arena_b[:, pb0 + 12288:pb0 + 16416].rearrange("p (n d) -> p n d", d=129), "mv")
        og_t = TT(arena_b[:, pb0 + 16416:pb0 + 16928], "og")
        xts = Ring([TT(arena_f[:, i * 2048:(i + 1) * 2048], "xt%d" % i) for i in range(2)])
        xn = TT(arena_f[:, 4096:6144], "xn")
        mix = [TT(arena_f[:, 6144 + i * 2048:6144 + (i + 1) * 2048], "mix%d" % i) for i in range(4)]
        gbt = TT(arena_f[:, 14336:16384], "gbt")
        HTt = TT(arena_f[:, 6144:10240], "HT")
        modrow = TT(arena_f[:, 12288:14336], "modrow", mix[3].b)
        ident = SB("ident", [128, 128]); onesf = SB("onesf", [128, 128]); ones_bf = SB("ones_bf", [128, 128], BF16)
        ident_bf = SB("ident_bf", [128, 128], BF16)
        maskn = SB("maskn", [128, 2, 128]); tri = SB("tri", [128, 2, 128])
        perm_bf = SB("perm_bf", [128, 2, 128], BF16)
        epsc = SB("epsc", [128, 1]); onec = SB("onec", [128, 1]); negc = SB("negc", [128, 1])
        rope_t = SB("rope_t", [128, 4, 512])
        silu_pad = SB("silu_pad", [128, 16, 128], BF16)
        gcolt = SB("gcolt", [128, 16])
        gb16 = SB("gb16_sb", [128, 1])
        condt = SB("condt", [128, 16, 2])
        modfm = SB("modfm", [128, 96, 2]); bfm = SB("bfm", [128, 96]); ngfm = SB("ngfm", [128, 4, 16])
        ABt = SB("ABt", [128, 4, 2, 16])
        brow = TT(arena_f[0:2, 4096:6144], "brow", xn.b); ngrow = TT(arena_f[0:2, 14336:16384], "ngrow", gbt.b)
        qkg = SB("qkg", [128, 2]); kgb = SB("kgb", [128, 128]); mlg = SB("mlg", [128, 4])
        dgc = SB("dgc", [128, 1]); dlt = SB("dlt", [128, 256]); lamc = SB("lamc", [128, 4]); minit = SB("minit", [128, 2])
        stat = SB("stat", [128, 16])
        TF = Ring([SB("tf%d" % i, [128, 512]) for i in range(5)])
        def g4get():
            t_ = TF.get()
            return TT(t_.t[0:4, :], "g4", t_.b)
        TB_ = Ring([SB("tb%d" % i, [128, 512], BF16) for i in range(6)])
        PTr = Ring([SB("pt%d" % i, [128, 512], BF16) for i in range(3)])
        ML = {}
        for dname in ('f', 'b'):
            for nm in ('R3',):
                ML[nm + dname] = SB("ml_%s%s" % (nm, dname), [128, 3, 128])
            for nm in ('GC', 'MPB', 'WEND', 'DEC', 'MROW'):
                ML[nm + dname] = SB("ml_%s%s" % (nm, dname), [128, 128])
        mlt = Ring([SB("mlt%d" % i, [128, 128]) for i in range(4)])
        MP = {nm: SB("mp_" + nm, [128, 128]) for nm in ('ig', 'lf', 'B', 'g', 'G', 'MM', 'sA', 'sB', 'mbe', 'mbp', 'meb')}
        mlc = Ring([SB("mlc%d" % i, [128, 2]) for i in range(6)])
        mlrow = Ring([SB("mlrow%d" % i, [1, 128]) for i in range(3)])
        r3s = Ring([SB("r3s%d" % i, [128, 3, 128]) for i in range(2)])
        mls = Ring([SB("mls%d" % i, [128, 256]) for i in range(4)])
        scb = Ring([SB("scb%d" % i, [128, 128], BF16) for i in range(3)])
        CN = SB("CN", [128, 129]); Cbf = SB("Cbf", [128, 128], BF16); NBb = SB("NBb", [128, 128], BF16)
        ps = [TT(st.enter_context(nc.psum_tensor("ps%d" % i, [128, 512], F32)), "ps%d" % i) for i in range(8)]

        def bufs(xs):
            return [x.b if isinstance(x, TT) else x for x in xs]

        def act(out, in_, func, r, w, **kw):
            P.op('act', lambda e: e.activation(out=out, in_=in_, func=func, **kw), bufs(r), bufs(w))

        def vtt(out, in0, in1, op, r, w):
            P.op('dve', lambda e: e.tensor_tensor(out=out, in0=in0, in1=in1, op=op), bufs(r), bufs(w))

        def vts(out, in0, s1, s2, op0, op1, r, w):
            if op1 is None:
                P.op('dve', lambda e: e.tensor_scalar(out=out, in0=in0, scalar1=s1, scalar2=None, op0=op0), bufs(r), bufs(w))
            else:
                P.op('dve', lambda e: e.tensor_scalar(out=out, in0=in0, scalar1=s1, scalar2=s2, op0=op0, op1=op1), bufs(r), bufs(w))

        def vstt(out, in0, scalar, in1, op0, op1, r, w):
            P.op('dve', lambda e: e.scalar_tensor_tensor(out=out, in0=in0, scalar=scalar, in1=in1, op0=op0, op1=op1), bufs(r), bufs(w))

        def vcp(out, in_, r, w):
            P.op('dve', lambda e: e.tensor_copy(out=out, in_=in_), bufs(r), bufs(w))

        def vrec(out, in_, r, w):
            P.op('dve', lambda e: e.reciprocal(out=out, in_=in_), bufs(r), bufs(w))

        def vset(out, val, w):
            P.op('dve', lambda e: e.memset(out, val), [], bufs(w))

        def mm(out, lhsT, rhs, r, w, start=True, stop=True):
            P.op('pe', lambda e: e.matmul(out, lhsT=lhsT, rhs=rhs, start=start, stop=stop), bufs(r), bufs(w))

        def trp(out, in_, idn, r, w):
            P.op('pe', lambda e: e.transpose(out=out, in_=in_, identity=idn), bufs(r), bufs(w))

        def dma(q, out, in_, r, w):
            P.dma(q, lambda e: e.dma_start(out=out, in_=in_), bufs(r), bufs(w))

        def rstd_from(ss_ap, n, cnt, rb, wb):
            act(ss_ap, ss_ap, AF.Ln, rb, wb, scale=1.0 / cnt, bias=epsc[:, 0:1])
            act(ss_ap, ss_ap, AF.Exp, rb, wb, scale=-0.5)

        def wload(src_ap, shape_elems, view):
            slot = WR.get()
            v = view(slot.t[:, 0:shape_elems])
            dma('pool', v, src_ap, [], [slot])
            return slot, v

        dma('sp', ident[:], c_ident[:, :], [], [ident])
        dma('pool', ident_bf[:], c_ident[:, :], [], [ident_bf])
        dma('sp', maskn[:], c_mask.rearrange("a p j -> p a j"), [], [maskn])
        dma('sp', tri[:], c_tri.rearrange("a p j -> p a j"), [], [tri])
        dma('pool', perm_bf[:], c_perm.rearrange("a p j -> p a j"), [], [perm_bf])
        dma('sp', condt[:], condT[:, :, :], [], [condt])
        vset(onesf[:], 1.0, [onesf]); vset(ones_bf[:], 1.0, [ones_bf]); vset(epsc[:], EPS, [epsc]); vset(onec[:], 1.0, [onec])
        vset(negc[:], NEG, [negc])
        t0 = TF.get()
        cflat = condt[:].rearrange("p k c -> p (k c)")
        act(t0[:, 0:32], cflat, AF.Exp, [condt], [t0], scale=-1.0)
        vts(t0[:, 0:32], t0[:, 0:32], 1.0, None, ALU.add, None, [t0], [t0])
        vrec(t0[:, 0:32], t0[:, 0:32], [t0], [t0])
        vset(silu_pad[:], 0.0, [silu_pad])
        vtt(silu_pad[:, :, 0:2], t0[:, 0:32].rearrange("p (k c) -> p k c", c=2), condt[:], ALU.mult, [t0, condt], [silu_pad])

        def norm_T(tiles, ci, which, dst):
            for tt_i, xt in enumerate(tiles):
                act(junk[:], xt[:], AF.Square, [xt], [junk, stat], accum_out=stat[:, 0:1])
                rstd_from(stat[:, 0:1], 1, float(D), [stat, epsc], [stat])
                vts(xn[:], xt[:], stat[:, 0:1], None, ALU.mult, None, [xt, stat], [xn])
                for cg in range(4):
                    pst = ps[cg % 2]
                    for c4 in range(4):
                        c = cg * 4 + c4
                        trp(pst[:, c4 * 128:(c4 + 1) * 128], xn[:, c * 128:(c + 1) * 128], ident[:], [xn, ident], [pst])
                    for c4 in range(4):
                        c = cg * 4 + c4
                        act(dst[:, c, tt_i * 128:(tt_i + 1) * 128], pst[:, c4 * 128:(c4 + 1) * 128], AF.Identity, [pst, ABt], [dst],
                            scale=ABt[:, which, ci, c:c + 1], bias=ABt[:, which + 1, ci, c:c + 1])

        def layer_setup(l):
            lam_init = 0.8 - 0.6 * math.exp(-0.3 * l)
            dma('sp', bfm[:], b_ada_fm[l], [], [bfm]); dma('sp', ngfm[:], ng_fm[l], [], [ngfm])
            dma('sp', qkg[:], qkg_col[l], [], [qkg]); dma('sp', kgb[:], kg_bc[l], [], [kgb]); dma('sp', gb16[:], gb16_in[l], [], [gb16])
            dma('sp', mlg[:], mlg_col[l], [], [mlg]); dma('sp', dgc[:], dg_col[l], [], [dgc]); dma('sp', dlt[:], dl_bc[l], [], [dlt])
            dma('sp', minit[:], minit_s[l], [], [minit])
            t1 = TF.get()
            vtt(t1[:, 0:64], dlt[:, 0:64], dlt[:, 64:128], ALU.mult, [dlt], [t1])
            vtt(t1[:, 64:128], dlt[:, 128:192], dlt[:, 192:256], ALU.mult, [dlt], [t1])
            P.op('dve', lambda e: e.tensor_reduce(out=lamc[:, 0:2], in_=t1[:, 0:128].rearrange("p (a b) -> p a b", a=2), axis=AX.X, op=ALU.add), bufs([t1]), bufs([lamc]))
            act(lamc[:, 0:2], lamc[:, 0:2], AF.Exp, [lamc], [lamc])
            vtt(lamc[:, 2:3], lamc[:, 0:1], lamc[:, 1:2], ALU.subtract, [lamc], [lamc])
            vts(lamc[:, 2:3], lamc[:, 2:3], -1.0, -lam_init, ALU.mult, ALU.add, [lamc], [lamc])
            vts(lamc[:, 3:4], dgc[:, 0:1], 1.0 - lam_init, None, ALU.mult, None, [dgc], [lamc])
            wv = w_ada[l].rearrange("(k p) n -> p k n", p=128)
            rot = 0
            import os as _os
            KSK = _os.environ.get('KSKIPGEMV', '')
            if KSK == '1':
                return
            for pi in range(48 if KSK != 'a' else 0):
                slot, v = wload(wv[:, :, pi * 256:(pi + 1) * 256], 4096, lambda a: a.rearrange("p (k n) -> p k n", k=16))
                for j in range(2):
                    cc = pi * 2 + j
                    pp = ps[rot % 4]; rot += 1
                    for k in range(16):
                        mm(pp[:, 0:128], v[:, k, j * 128:(j + 1) * 128], silu_pad[:, k, :], [slot, silu_pad], [pp], start=(k == 0), stop=(k == 15))
                    vcp(modfm[:, cc, :], pp[:, 0:2], [pp], [modfm])
            for ci in range(2):
                vtt(modfm[:, :, ci], modfm[:, :, ci], bfm[:], ALU.add, [modfm, bfm], [modfm])
                vstt(ABt[:, 0, ci, :], modfm[:, 16:32, ci], 1.0, ngfm[:, 0, :], ALU.add, ALU.mult, [modfm, ngfm], [ABt])
                vcp(ABt[:, 1, ci, :], modfm[:, 0:16, ci], [modfm], [ABt])
                vstt(ABt[:, 2, ci, :], modfm[:, 64:80, ci], 1.0, ngfm[:, 2, :], ALU.add, ALU.mult, [modfm, ngfm], [ABt])
                vcp(ABt[:, 3, ci, :], modfm[:, 48:64, ci], [modfm], [ABt])
            if KSK == 'b':
                return
            for which, (cc0, ngi) in enumerate(((32, 1), (80, 3))):
                for ci in range(2):
                    vtt(gcolt[:, :], modfm[:, cc0:cc0 + 16, ci], ngfm[:, ngi, :], ALU.mult, [modfm, ngfm], [gcolt])
                    for cg4 in range(4):
                        pb = ps[5 + (cg4 % 2)]
                        for c4 in range(4):
                            c = cg4 * 4 + c4
                            lt = mlt.get()
                            act(lt[:], onesf[:], AF.Identity, [onesf, gcolt], [lt], scale=gcolt[:, c:c + 1])
                            mm(pb[:, c4 * 128:(c4 + 1) * 128], lt[:], ident[:], [lt, ident], [pb])
                        tg = TF.get()
                        act(tg[:], pb[:, :], AF.Copy, [pb], [tg])
                        dma('sp', GBS[which, ci][:, cg4 * 512:(cg4 + 1) * 512], tg[:], [tg], [GBS])

        def store(dst_ap, src_tt, src_ap, dstb):
            dma('sp', dst_ap, src_ap, [src_tt], [dstb])

        def rope_block(l, pin, x_bf, pidx, tbl, dst_ap, dstb):
            pr = ps[7]
            mm(pr[:, :], perm_bf[:, pidx, :], x_bf[:], [perm_bf, x_bf], [pr])
            t1 = TF.get(); t2 = TF.get(); ob = TB_.get()
            vtt(t1[:], pin[1], rope_t[:, tbl, :], ALU.mult, [pin[0], rope_t], [t1])
            vtt(t2[:], pr[:, :], rope_t[:, tbl + 1, :], ALU.mult, [pr, rope_t], [t2])
            vtt(ob[:], t1[:], t2[:], ALU.add, [t1, t2], [ob])
            store(dst_ap, ob, ob[:], dstb)

        def phaseA(g, l, tb):
            T0 = tb * 512
            ci = 0 if g == 'p' else 1
            src = xin[g] if l == 0 else X[g].t
            srcb = [] if l == 0 else [X[g]]
            tiles = []
            for tt_i in range(4):
                xt = xts.get()
                dma('sp', xt[:], src[T0 + tt_i * 128:T0 + (tt_i + 1) * 128, :], srcb, [xt])
                tiles.append(xt)
                act(junk[:], xt[:], AF.Square, [xt], [junk, stat], accum_out=stat[:, 0:1])
                rstd_from(stat[:, 0:1], 1, float(D), [stat, epsc], [stat])
                vts(xn[:], xt[:], stat[:, 0:1], None, ALU.mult, None, [xt, stat], [xn])
                for cg in range(4):
                    pst = ps[cg % 2]
                    for c4 in range(4):
                        c = cg * 4 + c4
                        trp(pst[:, c4 * 128:(c4 + 1) * 128], xn[:, c * 128:(c + 1) * 128], ident[:], [xn, ident], [pst])
                    for c4 in range(4):
                        c = cg * 4 + c4
                        act(hnT[:, c, tt_i * 128:(tt_i + 1) * 128], pst[:, c4 * 128:(c4 + 1) * 128], AF.Identity, [pst, ABt], [hnT],
                            scale=ABt[:, 0, ci, c:c + 1], bias=ABt[:, 1, ci, c:c + 1])
            if g == 's':
                dma('sp', rope_t[:], c_rope[:, :, T0:T0 + 512].rearrange("a p t -> p a t"), [], [rope_t])
            import os as _os
            SUBA = int(_os.environ.get('KSUBA', '9'))
            if SUBA < 1:
                return
            wv = w_in[l].rearrange("(k p) n -> p k n", p=128)
            fm = [('aq', C_AQ + i * 256, i * 2) for i in range(4)] + [('ak', C_AK, 0)] + \
                 [('mq', C_MQ + i * 256, i * 2) for i in range(2)] + [('mk', C_MK + i * 256, i * 2) for i in range(2)] + \
                 [('mo', C_MO + i * 256, i * 2) for i in range(2)] + [('dq', C_DQ + i * 256, i * 2) for i in range(2)] + \
                 [('dk', C_DK + i * 256, i * 2) for i in range(2)]
            rot = 0
            for kind, c0, h0 in fm:
                slot, v = wload(wv[:, :, c0:c0 + 256], 4096, lambda a: a.rearrange("p (k n) -> p k n", k=16))
                for j in range(2):
                    h = h0 + j
                    pp = ps[2 + rot % 4]; rot += 1
                    for k in range(16):
                        mm(pp[:, :], v[:, k, j * 128:(j + 1) * 128], hnT[:, k, :], [slot, hnT], [pp], start=(k == 0), stop=(k == 15))
                    tsl = slice(T0, T0 + 512)
                    if kind in ('aq', 'ak'):
                        sq = TB_.get()
                        act(sq[:], pp[:, :], AF.Square, [pp], [sq])
                        pa = ps[6]
                        mm(pa[:, :], ones_bf[:], sq[:], [ones_bf, sq], [pa])
                        rs = TF.get()
                        act(rs[:], pa[:, :], AF.Ln, [pa, epsc], [rs], scale=1.0 / 128, bias=epsc[:, 0:1])
                        act(rs[:], rs[:], AF.Exp, [rs], [rs], scale=-0.5)
                        qg = TB_.get()
                        gi = 0 if kind == 'aq' else 1
                        vstt(qg[:], pp[:, :], qkg[:, gi:gi + 1], rs[:], ALU.mult, ALU.mult, [pp, qkg, rs], [qg])
                        dst = (QT[g] if kind == 'aq' else KT[g])
                        if g == 'p':
                            store(dst[h][:, tsl], qg, qg[:], dst)
                        else:
                            rope_block(l, (qg, qg[:]), qg, 0, 0, dst[h][:, tsl], dst)
                    elif kind in ('dq', 'dk'):
                        qb = TB_.get()
                        act(qb[:], pp[:, :], AF.Copy, [pp], [qb])
                        dst = (DQT[g] if kind == 'dq' else DKT[g])
                        if g == 'p':
                            store(dst[h][:, tsl], qb, qb[:], dst)
                        else:
                            rope_block(l, (qb, qb[:]), qb, 1, 2, dst[h][:, tsl], dst)
                    elif kind == 'mq':
                        qb = TB_.get()
                        act(qb[:], pp[:, :], AF.Copy, [pp], [qb])
                        store(MQT[g][h][:, tsl], qb, qb[:], MQT[g])
                    elif kind == 'mk':
                        qb = TB_.get()
                        act(qb[:], pp[:, :], AF.Copy, [pp], [qb], scale=SCALE)
                        store(MKT[g][h][:, tsl], qb, qb[:], MKT[g])
                    elif kind == 'mo':
                        e1 = TF.get(); qb = TB_.get()
                        act(e1[:], pp[:, :], AF.Exp, [pp], [e1], scale=-1.0)
                        vts(e1[:], e1[:], 1.0, None, ALU.add, None, [e1], [e1])
                        vrec(qb[:], e1[:], [e1], [qb])
                        store(OGT[g][h][:, tsl], qb, qb[:], OGT[g])
            if SUBA < 2:
                return
            slot, v = wload(wv[:, :, C_MG:C_MG + 128], 2048, lambda a: a.rearrange("p (k n) -> p k n", k=16))
            pg = ps[6]
            for tt_i in range(4):
                pp = ps[2 + rot % 4]; rot += 1
                for k in range(16):
                    mm(pp[:, 0:128], hnT[:, k, tt_i * 128:(tt_i + 1) * 128], v[:, k, :], [slot, hnT], [pp], start=(k == 0), stop=(k == 15))
                gt_ = mlt.get()
                act(gt_[:], pp[:, 0:128], AF.Copy, [pp], [gt_])
                trp(pg[:, tt_i * 128:(tt_i + 1) * 128], gt_[:], ident[:], [gt_, ident], [pg])
            xg = TF.get(); ax = TF.get(); mn = TF.get(); lf = TF.get()
            vts(xg[0:16, :], pg[0:16, :], gb16[0:16, 0:1], None, ALU.add, None, [pg, gb16], [xg])
            vstt(ax[0:16, :], xg[0:16, :], -1.0, xg[0:16, :], ALU.mult, ALU.max, [xg], [ax])
            act(ax[0:16, :], ax[0:16, :], AF.Exp, [ax], [ax], scale=-1.0)
            act(ax[0:16, :], ax[0:16, :], AF.Ln, [ax, onec], [ax], bias=onec[0:16, 0:1])
            vts(mn[0:16, :], xg[0:16, :], 0.0, None, ALU.min, None, [xg], [mn])
            vtt(lf[0:16, :], mn[0:16, :], ax[0:16, :], ALU.subtract, [mn, ax], [lf])
            c0 = T0 // 128
            for kind in range(4):
                res = lf if kind % 2 == 1 else xg
                dma('sp', GR[g][kind, c0:c0 + 4].rearrange("c h j -> h c j"), res[kind * 4:(kind + 1) * 4, :].rearrange("h (c j) -> h c j", j=128), [res], [GR[g]])
            if SUBA < 3:
                return
            tm = [('av', C_AV, 0), ('mk', C_MK, 0), ('mk', C_MK + 256, 256), ('mv', C_MV, 0), ('mv', C_MV + 256, 256),
                  ('dv', C_DV, 0), ('dv', C_DV + 256, 256)]
            if g == 'p':
                tm += [('ak', C_AK, 0), ('dk', C_DK, 0), ('dk', C_DK + 256, 256)]
            for kind, c0, o0 in tm:
                slot, v = wload(wv[:, :, c0:c0 + 256], 4096, lambda a: a.rearrange("p (k n) -> p k n", k=16))
                for tt_i in range(4):
                    pp = ps[2 + rot % 4]; rot += 1
                    for k in range(16):
                        mm(pp[:, 0:256], hnT[:, k, tt_i * 128:(tt_i + 1) * 128], v[:, k, :], [slot, hnT], [pp], start=(k == 0), stop=(k == 15))
                    r0 = T0 + tt_i * 128
                    rsl = slice(r0, r0 + 128)
                    seq, t0s = r0 // 256, r0 % 256
                    if kind in ('av', 'mk', 'mv', 'dv'):
                        qb = TB_.get()
                        act(qb[:, 0:256], pp[:, 0:256], AF.Copy, [pp], [qb], scale=(SCALE if kind == 'mk' else 1.0))
                        dst = {'av': VV, 'mk': MK, 'mv': MV, 'dv': DV}[kind][g]
                        store(dst[rsl, o0:o0 + 256], qb, qb[:, 0:256], dst)
                    if g == 'p' and kind in ('av', 'dv', 'dk'):
                        tf = TF.get()
                        vcp(tf[:, 0:256], pp[:, 0:256], [pp] + ([qb] if kind in ('av', 'dv') else []), [tf])
                        dst = {'av': o_gv, 'dv': o_dv, 'dk': o_dk}[kind]
                        dma('sp', dst[seq, l, t0s:t0s + 128, o0:o0 + 256], tf[:, 0:256], [tf], [bout])
                    if kind == 'ak':
                        tf = TF.get()
                        for hh in range(2):
                            act(junk[:, 0:128], pp[:, hh * 128:(hh + 1) * 128], AF.Square, [pp], [junk, stat], accum_out=stat[:, 2 + hh:3 + hh])
                        rstd_from(stat[:, 2:4], 2, 128.0, [stat, epsc], [stat])
                        for hh in range(2):
                            vstt(tf[:, hh * 128:(hh + 1) * 128], pp[:, hh * 128:(hh + 1) * 128], stat[:, 2 + hh:3 + hh], kgb[:], ALU.mult, ALU.mult, [pp, stat, kgb], [tf])
                        dma('sp', o_gk[seq, l, t0s:t0s + 128, :], tf[:, 0:256], [tf], [bout])

        def cache_prep(l):
            for (csrc, nh, dstT) in ((cgk, 2, KT['s']), (cdk, 4, DKT['s'])):
                for tt_i in range(2):
                    xt = xts.get()
                    dma('sp', xt[:, 0:nh * 128], csrc[l, tt_i * 128:(tt_i + 1) * 128, :], [], [xt])
                    for h in range(nh):
                        pst = ps[h % 2]
                        trp(pst[:, 0:128], xt[:, h * 128:(h + 1) * 128], ident[:], [xt, ident], [pst])
                        qb = TB_.get()
                        act(qb[:, 0:128], pst[:, 0:128], AF.Copy, [pst], [qb])
                        store(dstT[h][:, TS + tt_i * 128:TS + (tt_i + 1) * 128], qb, qb[:, 0:128], dstT)
            for (csrc, w, dstV) in ((cgv, 256, VV['s']), (cdv, 512, DV['s'])):
                for tt_i in range(2):
                    qb = TB_.get()
                    dma('pool', qb[:, 0:w], csrc[l, tt_i * 128:(tt_i + 1) * 128, :], [], [qb])
                    store(dstV[TS + tt_i * 128:TS + (tt_i + 1) * 128, 0:w], qb, qb[:, 0:w], dstV)

        def seqs(g):
            if g == 's':
                return [(0, TS, 0, TKS)]
            return [(i * 256, 256, i * 256, 256) for i in range(4)]

        def gqa(g, l):
            srot = 0
            for (q0, Tq, k0, Tk) in seqs(g):
                nkb = Tk // 128
                for kv in range(2):
                    dma('sp', ktile[:, 0:Tk], KT[g][kv][:, k0:k0 + Tk], [KT[g]], [ktile])
                    dma('sp', vtile[:, 0:nkb, :], VV[g][k0:k0 + Tk, kv * 128:(kv + 1) * 128].rearrange("(n p) d -> p n d", p=128), [VV[g]], [vtile])
                    for hh in range(4):
                        h = kv * 4 + hh
                        dma('sp', qA[:, 0:Tq], QT[g][h][:, q0:q0 + Tq], [QT[g]], [qA])
                        for qb in range(0, Tq, 512):
                            nq = min(512, Tq - qb)
                            O = ps[3 + 2 * ((qb // 512) % 2)]; Dn = ps[4 + 2 * ((qb // 512) % 2)]
                            for kb in range(nkb):
                                S = ps[srot % 3]; srot += 1
                                mm(S[:, 0:nq], ktile[:, kb * 128:(kb + 1) * 128], qA[:, qb:qb + nq], [ktile, qA], [S])
                                Pt = PTr.get()
                                act(Pt[:, 0:nq], S[:, 0:nq], AF.Exp, [S], [Pt], scale=SCALE)
                                mm(O[:, 0:nq], vtile[:, kb, :], Pt[:, 0:nq], [vtile, Pt], [O], start=(kb == 0), stop=(kb == nkb - 1))
                                mm(Dn[:, 0:nq], ones_bf[:], Pt[:, 0:nq], [ones_bf, Pt], [Dn], start=(kb == 0), stop=(kb == nkb - 1))
                            rc = TF.get(); ob = TB_.get()
                            vrec(rc[:, 0:nq], Dn[:, 0:nq], [Dn], [rc])
                            vtt(ob[:, 0:nq], O[:, 0:nq], rc[:, 0:nq], ALU.mult, [O, rc], [ob])
                            store(CATT[g][h][:, q0 + qb:q0 + qb + nq], ob, ob[:, 0:nq], CATT[g])

        def diffattn(g, l):
            srot = 0
            vset(qA[64:128, :], 0.0, [qA]); vset(qB[0:64, :], 0.0, [qB])
            for (q0, Tq, k0, Tk) in seqs(g):
                nkb = Tk // 128
                for h in range(4):
                    dma('sp', ktile[:, 0:Tk], DKT[g][h][:, k0:k0 + Tk], [DKT[g]], [ktile])
                    dma('sp', vtile[:, 0:nkb, :], DV[g][k0:k0 + Tk, h * 128:(h + 1) * 128].rearrange("(n p) d -> p n d", p=128), [DV[g]], [vtile])
                    dma('sp', qA[0:64, 0:Tq], DQT[g][h][0:64, q0:q0 + Tq], [DQT[g]], [qA])
                    dma('sp', qB[64:128, 0:Tq], DQT[g][h][64:128, q0:q0 + Tq], [DQT[g]], [qB])
                    for qb in range(0, Tq, 512):
                        nq = min(512, Tq - qb)
                        acc = [(ps[3], ps[4]), (ps[5], ps[6])]
                        for kb in range(nkb):
                            for mi, qsrc in enumerate((qA, qB)):
                                S = ps[srot % 3]; srot += 1
                                mm(S[:, 0:nq], ktile[:, kb * 128:(kb + 1) * 128], qsrc[:, qb:qb + nq], [ktile, qsrc], [S])
                                Pt = PTr.get()
                                act(Pt[:, 0:nq], S[:, 0:nq], AF.Exp, [S], [Pt], scale=DSCALE)
                                O, Dn = acc[mi]
                                mm(O[:, 0:nq], vtile[:, kb, :], Pt[:, 0:nq], [vtile, Pt], [O], start=(kb == 0), stop=(kb == nkb - 1))
                                mm(Dn[:, 0:nq], ones_bf[:], Pt[:, 0:nq], [ones_bf, Pt], [Dn], start=(kb == 0), stop=(kb == nkb - 1))
                        r1 = TF.get(); r2 = TF.get(); o1 = TF.get(); o2 = TF.get()
                        vrec(r1[:, 0:nq], acc[0][1][:, 0:nq], [acc[0][1]], [r1])
                        vrec(r2[:, 0:nq], acc[1][1][:, 0:nq], [acc[1][1]], [r2])
                        vtt(o1[:, 0:nq], acc[0][0][:, 0:nq], r1[:, 0:nq], ALU.mult, [acc[0][0], r1], [o1])
                        vtt(o2[:, 0:nq], acc[1][0][:, 0:nq], r2[:, 0:nq], ALU.mult, [acc[1][0], r2], [o2])
                        vstt(o1[:, 0:nq], o2[:, 0:nq], lamc[:, 2:3], o1[:, 0:nq], ALU.mult, ALU.add, [o2, lamc, o1], [o1])
                        sq = TB_.get()
                        act(sq[:, 0:nq], o1[:, 0:nq], AF.Square, [o1], [sq])
                        pa = ps[7]
                        mm(pa[:, 0:nq], ones_bf[:], sq[:, 0:nq], [ones_bf, sq], [pa])
                        act(r1[:, 0:nq], pa[:, 0:nq], AF.Ln, [pa, epsc], [r1], scale=1.0 / 128, bias=epsc[:, 0:1])
                        act(r1[:, 0:nq], r1[:, 0:nq], AF.Exp, [r1], [r1], scale=-0.5)
                        ob = TB_.get()
                        vstt(ob[:, 0:nq], o1[:, 0:nq], lamc[:, 3:4], r1[:, 0:nq], ALU.mult, ALU.mult, [o1, lamc, r1], [ob])
                        store(CATT[g][12 + h][:, q0 + qb:q0 + qb + nq], ob, ob[:, 0:nq], CATT[g])

        def scan128(src, op, rev, dstt):
            cur = src
            k = 1
            i = 0
            while k < 128:
                nxt = dstt if k == 64 else (MP['sA'] if i % 2 == 0 else MP['sB'])
                if not rev:
                    vtt(nxt[:, k:128], cur[:, k:128], cur[:, 0:128 - k], op, [cur], [nxt])
                    vcp(nxt[:, 0:k], cur[:, 0:k], [cur], [nxt])
                else:
                    vtt(nxt[:, 0:128 - k], cur[:, 0:128 - k], cur[:, k:128], op, [cur], [nxt])
                    vcp(nxt[:, 128 - k:128], cur[:, 128 - k:128], [cur], [nxt])
                cur = nxt
                k *= 2
                i += 1
            return dstt

        def ml_prep(g, l, t0, nch, di, micol):
            dn = 'fb'[di]
            rev = (di == 1)
            NP = nch * 4
            c0 = t0 // 128
            end = 0 if rev else 127
            ig = MP['ig']; lf = MP['lf']
            if NP < 128:
                vset(ig[:], 0.0, [ig]); vset(lf[:], 0.0, [lf])
            dma('sp', ig[0:NP, :], GR[g][di * 2, c0:c0 + nch].rearrange("c h j -> (c h) j"), [GR[g]], [ig])
            dma('sp', lf[0:NP, :], GR[g][di * 2 + 1, c0:c0 + nch].rearrange("c h j -> (c h) j"), [GR[g]], [lf])
            bc = scan128(lf, ALU.add, rev, MP['B'])
            pb = ps[0]
            mm(pb[:, 0:128], tri[:, di, :], bc[:, :], [tri, bc], [pb])
            boff = mlc.get()
            vcp(boff[:, 0:1], pb[:, end:end + 1], [pb], [boff])
            bi = 0
            Bt = MP['B']; gt = MP['g']
            vts(Bt[:], bc[:], boff[:, bi:bi + 1], None, ALU.add, None, [bc, boff], [Bt])
            vtt(gt[:], ig[:], Bt[:], ALU.subtract, [ig, Bt], [gt])
            Gl = scan128(gt, ALU.max, rev, MP['G'])
            slot = di * 2
            dma('sp', BNC[slot:slot + 1, :].rearrange("a p -> p a"), Gl[:, end:end + 1], [Gl], [BNC])
            row = mlrow.get()
            dma('sp', row[0:1, :], BNC[slot:slot + 1, :], [BNC], [row])
            if NP < 128:
                vset(row[0:1, NP:128], NEG, [row])
            k = 4
            cur = row
            while k < 128:
                nxt = mlrow.get()
                if not rev:
                    vtt(nxt[0:1, k:128], cur[0:1, k:128], cur[0:1, 0:128 - k], ALU.max, [cur], [nxt])
                    vcp(nxt[0:1, 0:k], cur[0:1, 0:k], [cur], [nxt])
                else:
                    vtt(nxt[0:1, 0:128 - k], cur[0:1, 0:128 - k], cur[0:1, k:128], ALU.max, [cur], [nxt])
                    vcp(nxt[0:1, 128 - k:128], cur[0:1, 128 - k:128], [cur], [nxt])
                cur = nxt
                k *= 2
            ex = mlrow.get()
            if not rev:
                vcp(ex[0:1, 4:128], cur[0:1, 0:124], [cur], [ex]); vset(ex[0:1, 0:4], NEG, [ex])
            else:
                vcp(ex[0:1, 0:124], cur[0:1, 4:128], [cur], [ex]); vset(ex[0:1, 124:128], NEG, [ex])
            dma('sp', BNC[slot + 1:slot + 2, :], ex[0:1, :], [ex], [BNC])
            mfl = mlc.get()
            dma('sp', mfl[:, 0:1], BNC[slot + 1:slot + 2, :].rearrange("a p -> p a"), [BNC], [mfl])
            vtt(mfl[:, 0:1], mfl[:, 0:1], micol[1], ALU.max, [mfl, micol[0]], [mfl])
            MMt = MP['MM']
            vts(MMt[:], Gl[:], mfl[:, 0:1], None, ALU.max, None, [Gl, mfl], [MMt])
            R3 = ML['R3' + dn]
            vts(R3[:, 0, :], MMt[:], -1.0, None, ALU.mult, None, [MMt], [R3])
            vcp(R3[:, 1, :], R3[:, 0, :], [R3], [R3])
            vstt(R3[:, 2, :], Bt[:], -1.0, R3[:, 0, :], ALU.mult, ALU.add, [Bt, R3], [R3])
            vtt(ML['MROW' + dn][:], Bt[:], MMt[:], ALU.add, [Bt, MMt], [ML['MROW' + dn]])
            mbe = MP['mbe']; mbp = MP['mbp']
            act(mbe[:], onesf[:], AF.Identity, [onesf, MMt], [mbe], scale=MMt[:, end:end + 1])
            act(mbp[:], onesf[:], AF.Identity, [onesf, mfl], [mbp], scale=mfl[:, 0:1])
            pt = ps[1]
            trp(pt[:, 0:128], gt[:], ident[:], [gt, ident], [pt])
            trp(pt[:, 128:256], mbe[:], ident[:], [mbe, ident], [pt])
            trp(pt[:, 256:384], mbp[:], ident[:], [mbp, ident], [pt])
            meb = MP['meb']
            act(ML['GC' + dn][:], pt[:, 0:128], AF.Copy, [pt], [ML['GC' + dn]])
            act(meb[:], pt[:, 128:256], AF.Copy, [pt], [meb])
            act(ML['MPB' + dn][:], pt[:, 256:384], AF.Copy, [pt], [ML['MPB' + dn]])
            w1 = MP['sA']; w2 = MP['sB']
            vtt(w1[:], pt[:, 0:128], meb[:], ALU.subtract, [pt, meb], [w1])
            act(ML['WEND' + dn][:], w1[:], AF.Exp, [w1], [ML['WEND' + dn]])
            vtt(w2[:], pt[:, 256:384], meb[:], ALU.subtract, [pt, meb], [w2])
            act(ML['DEC' + dn][:], w2[:], AF.Exp, [w2], [ML['DEC' + dn]])

        def mlstm(g, l):
            for si, (t0, Tq, _k0, _tk) in enumerate(seqs(g)):
                nch = Tq // 128
                for di in range(2):
                    micol = (minit, minit[:, di:di + 1]) if g == 's' else (negc, negc[:, 0:1])
                    ml_prep(g, l, t0, nch, di, micol)
                for h in range(4):
                    dma('sp', mkt_t[:, 0:Tq], MKT[g][h][:, t0:t0 + Tq], [MKT[g]], [mkt_t])
                    dma('sp', mqt_t[:, 0:Tq], MQT[g][h][:, t0:t0 + Tq], [MQT[g]], [mqt_t])
                    dma('sp', mk_t[:, 0:nch, :], MK[g][t0:t0 + Tq, h * 128:(h + 1) * 128].rearrange("(n p) d -> p n d", p=128), [MK[g]], [mk_t])
                    dma('sp', mv_t[:, 0:nch, 0:128], MV[g][t0:t0 + Tq, h * 128:(h + 1) * 128].rearrange("(n p) d -> p n d", p=128), [MV[g]], [mv_t])
                    vset(mv_t[:, 0:nch, 128:129], 1.0, [mv_t])
                    for di in range(2):
                        dn = 'fb'[di]
                        rev = (di == 1)
                        R3 = ML['R3' + dn]; GC = ML['GC' + dn]; MPB = ML['MPB' + dn]; WEND = ML['WEND' + dn]; DEC = ML['DEC' + dn]
                        if g == 's':
                            dma('sp', CN[:, 0:128], stC[l, di, h], [], [CN])
                            dma('sp', CN[:, 128:129], stn[l, di, h], [], [CN])
                        else:
                            vset(CN[:], 0.0, [CN])
                        act(Cbf[:], CN[:, 0:128], AF.Copy, [CN], [Cbf])
                        act(NBb[:], onesf[:], AF.Identity, [onesf, CN], [NBb], scale=CN[:, 128:129])
                        order = range(nch - 1, -1, -1) if rev else range(nch)
                        for c in order:
                            p = c * 4 + h
                            cs = slice(c * 128, (c + 1) * 128)
                            rs3 = r3s.get()
                            vts(rs3[:].rearrange("p a j -> p (a j)"), R3[:].rearrange("p a j -> p (a j)"), ident[:, p:p + 1], None, ALU.mult, None, [R3, ident], [rs3])
                            nb = ps[2 + (c % 2) * 3]; stp = ps[3 + (c % 2) * 3]; q4 = ps[4 + (c % 2) * 3]; cu = ps[1]
                            mm(nb[:, 0:128], ident[:], maskn[:, di, :], [ident, maskn], [nb], start=True, stop=False)
                            mm(nb[:, 0:128], onesf[:], rs3[:, 0, :], [onesf, rs3], [nb], start=False, stop=True)
                            mm(nb[:, 128:384], onesf[:], rs3[:, 1:3, :].rearrange("p a j -> p (a j)"), [onesf, rs3], [nb])
                            mm(stp[:, 0:128], mkt_t[:, cs], mqt_t[:, cs], [mkt_t, mqt_t], [stp])
                            DT = mlt.get(); AI = mlt.get(); EM = mlt.get()
                            act(DT[:], nb[:, 0:128], AF.Exp, [nb, GC], [DT], bias=GC[:, p:p + 1])
                            act(AI[:], nb[:, 128:256], AF.Exp, [nb, MPB], [AI], bias=MPB[:, p:p + 1])
                            act(EM[:], nb[:, 256:384], AF.Exp, [nb], [EM])
                            sc = scb.get()
                            vtt(sc[:], stp[:, 0:128], DT[:], ALU.mult, [stp, DT], [sc])
                            mm(q4[:, 0:128], mv_t[:, c, 0:128], sc[:], [mv_t, sc], [q4])
                            mm(q4[:, 128:256], ones_bf[:], sc[:], [ones_bf, sc], [q4])
                            mm(q4[:, 256:384], Cbf[:], mqt_t[:, cs], [Cbf, mqt_t], [q4])
                            mm(q4[:, 384:512], NBb[:], mqt_t[:, cs], [NBb, mqt_t], [q4])
                            tmp = mls.get(); nd = mls.get()
                            vtt(tmp[:, 0:128], q4[:, 256:384], AI[:], ALU.mult, [q4, AI], [tmp])
                            vtt(tmp[:, 128:256], q4[:, 384:512], AI[:], ALU.mult, [q4, AI], [tmp])
                            vtt(nd[:], q4[:, 0:256], tmp[:], ALU.add, [q4, tmp], [nd])
                            vstt(tmp[:, 0:128], nd[:, 128:256], -1.0, nd[:, 128:256], ALU.mult, ALU.max, [nd], [tmp])
                            vtt(tmp[:, 0:128], tmp[:, 0:128], EM[:], ALU.max, [tmp, EM], [tmp])
                            vrec(tmp[:, 0:128], tmp[:, 0:128], [tmp], [tmp])
                            if di == 0:
                                vtt(HTt[:, cs], nd[:, 0:128], tmp[:, 0:128], ALU.mult, [nd, tmp], [HTt])
                            else:
                                vtt(tmp[:, 128:256], nd[:, 0:128], tmp[:, 0:128], ALU.mult, [nd, tmp], [tmp])
                                vtt(HTt[:, cs], HTt[:, cs], tmp[:, 128:256], ALU.add, [HTt, tmp], [HTt])
                            kw = scb.get()
                            vts(kw[:], mk_t[:, c, :], WEND[:, p:p + 1], None, ALU.mult, None, [mk_t, WEND], [kw])
                            mm(cu[:, 0:129], kw[:], mv_t[:, c, :], [kw, mv_t], [cu])
                            vstt(CN[:], CN[:], DEC[:, p:p + 1], cu[:, 0:129], ALU.mult, ALU.add, [CN, DEC, cu], [CN])
                            act(Cbf[:], CN[:, 0:128], AF.Copy, [CN], [Cbf])
                            act(NBb[:], onesf[:], AF.Identity, [onesf, CN], [NBb], scale=CN[:, 128:129])
                        if g == 'p':
                            dma('sp', o_C[si, l, di, h], CN[:, 0:128], [CN], [bout])
                            dma('sp', o_n[si, l, di, h], CN[:, 128:129], [CN], [bout])
                            if h == 3:
                                cl = 0 if rev else nch - 1
                                e_ = 0 if rev else 127
                                dma('sp', o_m[si, l, di], ML['MROW' + dn][cl * 4:cl * 4 + 4, e_:e_ + 1], [ML['MROW' + dn]], [bout])
                    for qb in range(0, Tq, 512):
                        nq = min(512, Tq - qb)
                        sq = TB_.get()
                        act(sq[:, 0:nq], HTt[:, qb:qb + nq], AF.Square, [HTt], [sq])
                        pa = ps[7]
                        mm(pa[:, 0:nq], ones_bf[:], sq[:, 0:nq], [ones_bf, sq], [pa])
                        rs = TF.get(); o1 = TF.get(); ob = TB_.get()
                        act(rs[:, 0:nq], pa[:, 0:nq], AF.Ln, [pa, epsc], [rs], scale=1.0 / 128, bias=epsc[:, 0:1])
                        act(rs[:, 0:nq], rs[:, 0:nq], AF.Exp, [rs], [rs], scale=-0.5)
                        vstt(o1[:, 0:nq], HTt[:, qb:qb + nq], mlg[:, h:h + 1], rs[:, 0:nq], ALU.mult, ALU.mult, [HTt, mlg, rs], [o1])
                        dma('sp', og_t[:, 0:nq], OGT[g][h][:, t0 + qb:t0 + qb + nq], [OGT[g]], [og_t])
                        vtt(ob[:, 0:nq], o1[:, 0:nq], og_t[:, 0:nq], ALU.mult, [o1, og_t], [ob])
                        store(CATT[g][8 + h][:, t0 + qb:t0 + qb + nq], ob, ob[:, 0:nq], CATT[g])

        def sandwich(l, g, ci, which, srcs, T0, resid_src, resid_b, dst_ap_fn, dstb, extra=None):
            dma('sp', gbt[:], GBS[which, ci], [GBS], [gbt])
            for tt_i in range(4):
                m_ = srcs[tt_i]
                act(junk[:], m_[:], AF.Square, [m_], [junk, stat], accum_out=stat[:, 4:5])
                rstd_from(stat[:, 4:5], 1, float(D), [stat, epsc], [stat])
                vstt(m_[:], m_[:], stat[:, 4:5], gbt[:], ALU.mult, ALU.mult, [m_, stat, gbt], [m_])
                xt = xts.get()
                dma('sp', xt[:], resid_src[T0 + tt_i * 128:T0 + (tt_i + 1) * 128, :], resid_b, [xt])
                vtt(m_[:], m_[:], xt[:], ALU.add, [m_, xt], [m_])
                dma('sp', dst_ap_fn(tt_i), m_[:], [m_], [dstb])

        def phaseC(g, l, tb):
            T0 = tb * 512
            ci = 0 if g == 'p' else 1
            catT = hnT
            dma('sp', catT[:], CATT[g][:, :, T0:T0 + 512].rearrange("k p t -> p k t"), [CATT[g]], [catT])
            wv = w_out[l].rearrange("(k p) n -> p k n", p=128)
            rot = 0
            for pc in range(8):
                slot, v = wload(wv[:, :, pc * 256:(pc + 1) * 256], 4096, lambda a: a.rearrange("p (k n) -> p k n", k=16))
                for tt_i in range(4):
                    pp = ps[rot % 4]; rot += 1
                    for k in range(16):
                        mm(pp[:, 0:256], catT[:, k, tt_i * 128:(tt_i + 1) * 128], v[:, k, :], [slot, catT], [pp], start=(k == 0), stop=(k == 15))
                    act(mix[tt_i][:, pc * 256:(pc + 1) * 256], pp[:, 0:256], AF.Copy, [pp], [mix[tt_i]])
            src = xin[g] if l == 0 else X[g].t
            srcb = [] if l == 0 else [X[g]]
            sandwich(l, g, ci, 0, mix, T0, src, srcb, lambda i: X1[g][T0 + i * 128:T0 + (i + 1) * 128, :], X1[g])
            for tt_i in range(4):
                xt = mix[tt_i]
                act(junk[:], xt[:], AF.Square, [xt], [junk, stat], accum_out=stat[:, 5:6])
                rstd_from(stat[:, 5:6], 1, float(D), [stat, epsc], [stat])
                vts(xn[:], xt[:], stat[:, 5:6], None, ALU.mult, None, [xt, stat], [xn])
                for cg in range(4):
                    pst = ps[4 + cg % 2]
                    for c4 in range(4):
                        c = cg * 4 + c4
                        trp(pst[:, c4 * 128:(c4 + 1) * 128], xn[:, c * 128:(c + 1) * 128], ident[:], [xn, ident], [pst])
                    for c4 in range(4):
                        c = cg * 4 + c4
                        act(hnT[:, c, tt_i * 128:(tt_i + 1) * 128], pst[:, c4 * 128:(c4 + 1) * 128], AF.Identity, [pst, ABt], [hnT],
                            scale=ABt[:, 2, ci, c:c + 1], bias=ABt[:, 3, ci, c:c + 1])
            w1v = w_ff1[l].rearrange("(k p) n -> p k n", p=128)
            w2v = w_ff2[l].rearrange("(k p) n -> p k n", p=128)
            for fh in range(2):
                for pc in range(16):
                    col0 = fh * 4096 + pc * 256
                    slot, v = wload(w1v[:, :, col0:col0 + 256], 4096, lambda a: a.rearrange("p (k n) -> p k n", k=16))
                    for j in range(2):
                        fc = pc * 2 + j
                        pp = ps[rot % 4]; rot += 1
                        for k in range(16):
                            mm(pp[:, :], v[:, k, j * 128:(j + 1) * 128], hnT[:, k, :], [slot, hnT], [pp], start=(k == 0), stop=(k == 15))
                        xs_ = TB_.get()
                        act(xs_[:], pp[:, :], AF.Copy, [pp], [xs_])
                        vstt(hT[:, fc, :], pp[:, :], 0.0, xs_[:], ALU.max, ALU.mult, [pp, xs_], [hT])
                for half in range(2):
                    for kg in range(8):
                        r0 = fh * 32 + kg * 4
                        slot, v = wload(w2v[:, r0:r0 + 4, half * 1024:(half + 1) * 1024], 4096, lambda a: a.rearrange("p (k n) -> p k n", k=4))
                        for tt_i in range(4):
                            for cgk in range(2):
                                pp = ps[tt_i * 2 + cgk]
                                for k4 in range(4):
                                    mm(pp[:, :], hT[:, kg * 4 + k4, tt_i * 128:(tt_i + 1) * 128], v[:, k4, cgk * 512:(cgk + 1) * 512], [slot, hT], [pp],
                                       start=(kg == 0 and k4 == 0), stop=(kg == 7 and k4 == 3))
                    for tt_i in range(4):
                        for cgk in range(2):
                            pp = ps[tt_i * 2 + cgk]
                            dsl = mix[tt_i][:, half * 1024 + cgk * 512:half * 1024 + (cgk + 1) * 512]
                            if fh == 0:
                                act(dsl, pp[:, :], AF.Copy, [pp], [mix[tt_i]])
                            else:
                                vtt(dsl, pp[:, :], dsl, ALU.add, [pp, mix[tt_i]], [mix[tt_i]])
            last = (l == nl_run - 1)
            dst = yout[g] if last else X[g].t
            dstb = bout if last else X[g]
            sandwich(l, g, ci, 1, mix, T0, X1[g].t, [X1[g]], lambda i: dst[T0 + i * 128:T0 + (i + 1) * 128, :], dstb)

        import os as _os
        STG = int(_os.environ.get("KSTAGE", "9"))
        for l in range(nl_run):
            layer_setup(l)
            if STG >= 2:
                for tb in range(TG['p'] // 512):
                    phaseA('p', l, tb)
            if STG >= 3:
                K3 = _os.environ.get('K3', 'ab')
                if 'a' in K3:
                    for tb in range(int(_os.environ.get('K3N', '8'))):
                        phaseA('s', l, tb)
                if 'b' in K3:
                    cache_prep(l)
            P.barrier()
            if STG >= 4:
                for g in 'ps':
                    gqa(g, l)
            P.barrier()
            if STG >= 5:
                for g in 'ps':
                    diffattn(g, l)
            P.barrier()
            if STG >= 6:
                for g in 'ps':
                    mlstm(g, l)
            P.barrier()
            if STG >= 7:
                for g in 'ps':
                    for tb in range(TG[g] // 512):
                        phaseC(g, l, tb)
            P.barrier()
        with nc.allow_low_precision(reason="bf16 matmul operands by design"):
            P.emit()
    return nc


def _rope_tables():
    t = np.arange(TS)
    row = (t // 64).astype(np.float32); col = (t % 64).astype(np.float32)

    def ang(nf):
        inv = (np.float32(10000.0) ** (-np.arange(nf, dtype=np.float32) / np.float32(nf))).astype(np.float32)
        return np.concatenate([row[:, None] * inv, col[:, None] * inv], axis=-1).astype(np.float32)
    a1 = ang(32); a2 = ang(16)
    out = np.zeros((4, 128, TS), np.float32)
    for d in range(128):
        out[0, d] = np.cos(a1[:, d % 64]); out[1, d] = np.sin(a1[:, d % 64]) * (-1.0 if d < 64 else 1.0)
        dd = d % 64
        out[2, d] = np.cos(a2[:, dd % 32]); out[3, d] = np.sin(a2[:, dd % 32]) * (-1.0 if dd < 32 else 1.0)
    return out


def _consts():
    ident = np.eye(128, dtype=np.float32)
    s = np.arange(128)[:, None]; j = np.arange(128)[None, :]
    mask = np.stack([np.where(s <= j, 0.0, NEG), np.where(s >= j, 0.0, NEG)]).astype(np.float32)
    pc = np.arange(128) // 4; ph = np.arange(128) % 4
    same = ph[:, None] == ph[None, :]
    tri = np.stack([(same & (pc[:, None] < pc[None, :])), (same & (pc[:, None] > pc[None, :]))]).astype(np.float32)
    sel = np.zeros((2, 128, 128), np.float32); sel[0, 0, :] = 1.0; sel[1, 1, :] = 1.0
    perm = np.zeros((2, 128, 128), np.float32)
    for m in range(128):
        perm[0, (m + 64) % 128, m] = 1.0
        mm_ = m % 64
        perm[1, (m - mm_) + (mm_ + 32) % 64, m] = 1.0
    return ident, mask, tri, sel, perm


_CACHE = {}


def kernel(x_prompt, x_sample, c, cache_gqa_k, cache_gqa_v, cache_diff_k, cache_diff_v,
           state_mlstm_C, state_mlstm_n, state_mlstm_m, c_ctx, w_ada, b_ada, norm_gain,
           w_in, w_out, qk_gain, mlstm_gate_bias, mlstm_head_gain, diff_lambda,
           diff_head_gain, w_ff1, w_ff2, _nl_run=NL):
    f = lambda a: np.ascontiguousarray(np.asarray(a, dtype=np.float32))
    x_prompt, x_sample, c, c_ctx = f(x_prompt), f(x_sample), f(c), f(c_ctx)
    w_ada, w_in, w_out, w_ff1, w_ff2 = f(w_ada), f(w_in), f(w_out), f(w_ff1), f(w_ff2)
    b_ada, norm_gain, qk_gain = f(b_ada), f(norm_gain), f(qk_gain)
    gate_bias, ml_gain, dlam, dgain = f(mlstm_gate_bias), f(mlstm_head_gain), f(diff_lambda), f(diff_head_gain)
    if _nl_run not in _CACHE:
        _CACHE[_nl_run] = build_program(_nl_run)
    nc = _CACHE[_nl_run]
    ident, mask, tri, sel, perm = _consts()
    rope = _rope_tables()
    shared = {
        "w_ada": w_ada, "w_in": w_in, "w_out": w_out, "w_ff1": w_ff1, "w_ff2": w_ff2,
        "b_ada_fm": f(b_ada.reshape(NL, 96, 128).transpose(0, 2, 1)),
        "b_ada_rows": f(np.stack([np.stack([np.broadcast_to(b_ada[l, 2 * D:3 * D], (2, D)), np.broadcast_to(b_ada[l, 5 * D:6 * D], (2, D))]) for l in range(NL)])),
        "ng_fm": f(norm_gain.reshape(NL, 4, 16, 128).transpose(0, 3, 1, 2)),
        "ng_rows": f(np.stack([np.stack([np.broadcast_to(norm_gain[l, 1], (2, D)), np.broadcast_to(norm_gain[l, 3], (2, D))]) for l in range(NL)])),
        "qkg_col": f(qk_gain.transpose(0, 2, 1)),
        "kg_bc": f(np.broadcast_to(qk_gain[:, 1][:, None, :], (NL, 128, 128))),
        "mlg_col": f(ml_gain.transpose(0, 2, 1)),
        "dg_col": f(dgain[:, :, None]),
        "dl_bc": f(np.broadcast_to(dlam.reshape(NL, 1, 256), (NL, 128, 256))),
        "c_ident": ident, "c_mask": mask, "c_tri": tri, "c_sel": sel, "c_perm": perm, "c_rope": rope,
    }
    gb16 = np.zeros((NL, 128, 1), np.float32)
    for l in range(NL):
        gb16[l, 0:16, 0] = gate_bias[l].reshape(16)
    shared["gb16"] = gb16
    in_maps = []
    for core in range(8):
        b = core // 4
        cond = np.stack([c_ctx, c[b]], axis=0)
        m = dict(shared)
        m["xp"] = f(x_prompt[core * 4:(core + 1) * 4].reshape(TP, D))
        m["xs"] = f(x_sample[b])
        m["condT"] = f(cond.reshape(2, 16, 128).transpose(2, 1, 0))
        m["cgk"] = f(np.asarray(cache_gqa_k)[b].reshape(NL, 256, 256)); m["cgv"] = f(np.asarray(cache_gqa_v)[b].reshape(NL, 256, 256))
        m["cdk"] = f(np.asarray(cache_diff_k)[b].reshape(NL, 256, 512)); m["cdv"] = f(np.asarray(cache_diff_v)[b].reshape(NL, 256, 512))
        m["stC"] = f(np.asarray(state_mlstm_C)[b]); m["stn"] = f(np.asarray(state_mlstm_n)[b][..., None])
        sm = np.asarray(state_mlstm_m, dtype=np.float32)[b]
        m["minit_s"] = f(np.tile(sm.transpose(0, 2, 1), (1, 32, 1)))
        in_maps.append(m)
    res = run_bass_kernel_spmd(nc, in_maps, core_ids=list(range(8)))
    R = res.results
    y_prompt = np.concatenate([np.asarray(R[i]["yp"]).reshape(4, 256, D) for i in range(8)], axis=0).astype(np.float32)
    y_sample = np.stack([np.asarray(R[0]["ys"]), np.asarray(R[4]["ys"])], axis=0).astype(np.float32)
    cat = lambda k, shp: np.concatenate([np.asarray(R[i][k]).reshape(shp) for i in range(8)], axis=0).astype(np.float32)
    ngk = cat("ngk", (4, NL, 256, 2, 128)); ngv = cat("ngv", (4, NL, 256, 2, 128))
    ndk = cat("ndk", (4, NL, 256, 4, 128)); ndv = cat("ndv", (4, NL, 256, 4, 128))
    nC = cat("nC", (4, NL, 2, 4, 128, 128)); nn = cat("nn", (4, NL, 2, 4, 128)); nm = cat("nm", (4, NL, 2, 4))
    return (y_prompt, y_sample, ngk, ngv, ndk, ndv, nC, nn, nm)
```
